# Optimizing a Trainium2 kernel written in Bass

```python
import jax, jax.numpy as jnp
from jax import lax
import numpy as np

D_MODEL = 1024
BATCH = 4
SEQ = 4096
DEPTH = 4

RWKV_HEADS = 4
RWKV_HEAD_DIM = 64
RWKV_WIDTH = RWKV_HEADS * RWKV_HEAD_DIM
DECAY_LORA = 64
AAA_LORA = 64
GATE_LORA = 128
RWKV_LN_EPS = 64e-5

FOX_HEADS = 4
FOX_HEAD_DIM = 64
FOX_WIDTH = FOX_HEADS * FOX_HEAD_DIM

MLA_HEADS = 4
QK_NOPE_DIM = 128
QK_ROPE_DIM = 64
V_HEAD_DIM = 128
Q_LORA_RANK = 384
KV_LORA_RANK = 256
MLA_WIDTH = MLA_HEADS * V_HEAD_DIM
ROPE_THETA = 10000.0

N_BRANCHES = 3
D_FF = 4 * D_MODEL
Q_BLOCK = 128
NORM_EPS = 1e-6
MASK_VALUE = -1e30

RWKV_COLS = 3 * RWKV_WIDTH + DECAY_LORA + AAA_LORA + GATE_LORA
FOX_COLS = 3 * FOX_WIDTH + FOX_HEADS
MLA_COLS = Q_LORA_RANK + KV_LORA_RANK + QK_ROPE_DIM
GATE_COLS = N_BRANCHES * D_MODEL
IN_COLS = RWKV_COLS + FOX_COLS + MLA_COLS + GATE_COLS

kernel_name = 'hybrid_rwkv7_fox_mla_gated_decoder'


def rms_norm(x, gain, eps=NORM_EPS):
    xf = x.astype(jnp.float32)
    y = xf * lax.rsqrt(jnp.mean(xf * xf, axis=-1, keepdims=True) + eps)
    return y.astype(x.dtype) * gain


def token_shift(z):
    return jnp.pad(z, ((0, 0), (1, 0), (0, 0)))[:, :-1]


def apply_rope(x, cos, sin):
    x1, x2 = jnp.split(x, 2, axis=-1)
    return jnp.concatenate([x1 * cos - x2 * sin, x1 * sin + x2 * cos], axis=-1).astype(x.dtype)


def causal_block_attention(q, k, v, scale, log_forget_cum=None):
    B, S, H, _ = q.shape
    k_pos = jnp.arange(S)
    cum_bhs = None if log_forget_cum is None else jnp.transpose(log_forget_cum, (0, 2, 1))

    def one_block(i):
        start = i * Q_BLOCK
        qb = lax.dynamic_slice_in_dim(q, start, Q_BLOCK, axis=1)
        s = jnp.einsum('bqhd,bkhd->bhqk', qb, k, preferred_element_type=jnp.float32) * scale
        if cum_bhs is not None:
            cq = lax.dynamic_slice_in_dim(cum_bhs, start, Q_BLOCK, axis=2)
            s = s + cq[..., :, None] - cum_bhs[..., None, :]
        q_pos = start + jnp.arange(Q_BLOCK)
        s = jnp.where(k_pos[None, :] <= q_pos[:, None], s, MASK_VALUE)
        p = jax.nn.softmax(s, axis=-1).astype(v.dtype)
        return jnp.einsum('bhqk,bkhd->bqhd', p, v)

    out = lax.map(one_block, jnp.arange(S // Q_BLOCK))
    return jnp.moveaxis(out, 0, 1).reshape(B, S, H, v.shape[-1])


def rwkv7_mix(z, w0, w_decay_up, a0, w_aaa_up, w_gate_up, k_k, k_a, r_k, ln_g, ln_b):
    B, S, _ = z.shape
    offs = np.cumsum([RWKV_WIDTH, RWKV_WIDTH, RWKV_WIDTH, DECAY_LORA, AAA_LORA]).tolist()
    r, k, v, wd, ad, gd = jnp.split(z, offs, axis=-1)
    w_log = -jax.nn.softplus(-(w0 + jnp.tanh(wd) @ w_decay_up)) - 0.5
    decay = jnp.exp(-jnp.exp(w_log.astype(jnp.float32)))
    a = jax.nn.sigmoid(a0 + ad @ w_aaa_up)
    g = jax.nn.sigmoid(gd) @ w_gate_up
    kk = k * k_k
    k = k * (1 + (a - 1) * k_a)
    heads = lambda t: t.reshape(B, S, RWKV_HEADS, RWKV_HEAD_DIM).astype(jnp.float32)
    r_h, k_h, v_h, a_h, w_h, kk_h = map(heads, (r, k, v, a, decay, kk))
    kk_h = kk_h / jnp.maximum(jnp.linalg.norm(kk_h, axis=-1, keepdims=True), 1e-12)

    def step(state, inp):
        r_t, w_t, k_t, v_t, kk_t, a_t = inp
        sa = jnp.einsum('bhvk,bhk->bhv', state, -kk_t)
        state = (state * w_t[:, :, None, :] + sa[..., None] * (kk_t * a_t)[:, :, None, :]
                 + v_t[..., None] * k_t[:, :, None, :])
        return state, jnp.einsum('bhvk,bhk->bhv', state, r_t)

    state0 = jnp.zeros((B, RWKV_HEADS, RWKV_HEAD_DIM, RWKV_HEAD_DIM), jnp.float32)
    xs = tuple(jnp.moveaxis(t, 1, 0) for t in (r_h, w_h, k_h, v_h, kk_h, a_h))
    _, y = lax.scan(step, state0, xs)
    y = jnp.moveaxis(y, 0, 1)
    mean = jnp.mean(y, -1, keepdims=True)
    var = jnp.mean(jnp.square(y - mean), -1, keepdims=True)
    y = ((y - mean) * lax.rsqrt(var + RWKV_LN_EPS)).reshape(B, S, RWKV_WIDTH) * ln_g + ln_b
    bonus = jnp.sum(r_h * k_h * r_k, -1, keepdims=True) * v_h
    y = (y + bonus.reshape(B, S, RWKV_WIDTH)) * g
    return y.astype(z.dtype)


def fox_mix(z, b_forget):
    B, S, _ = z.shape
    q, k, v, f = jnp.split(z, [FOX_WIDTH, 2 * FOX_WIDTH, 3 * FOX_WIDTH], axis=-1)
    heads = lambda t: t.reshape(B, S, FOX_HEADS, FOX_HEAD_DIM)
    log_f = jax.nn.log_sigmoid((f + b_forget).astype(jnp.float32))
    cum = jnp.cumsum(log_f, axis=1)
    o = causal_block_attention(heads(q), heads(k), heads(v), FOX_HEAD_DIM ** -0.5, cum)
    return o.reshape(B, S, FOX_WIDTH)


def mla_mix(z, q_norm_g, w_q_up, kv_norm_g, w_kv_up, cos, sin):
    B, S, _ = z.shape
    q_lat, kv_lat, k_pe = jnp.split(z, [Q_LORA_RANK, Q_LORA_RANK + KV_LORA_RANK], axis=-1)
    q = (rms_norm(q_lat, q_norm_g) @ w_q_up).reshape(B, S, MLA_HEADS, QK_NOPE_DIM + QK_ROPE_DIM)
    kv = (rms_norm(kv_lat, kv_norm_g) @ w_kv_up).reshape(B, S, MLA_HEADS, QK_NOPE_DIM + V_HEAD_DIM)
    q_nope, q_pe = jnp.split(q, [QK_NOPE_DIM], axis=-1)
    k_nope, v = jnp.split(kv, [QK_NOPE_DIM], axis=-1)
    q_pe = apply_rope(q_pe, cos[:, :, None, :], sin[:, :, None, :])
    k_pe = apply_rope(k_pe, cos, sin)[:, :, None, :]
    q_full = jnp.concatenate([q_nope, q_pe], axis=-1)
    k_full = jnp.concatenate([k_nope, jnp.broadcast_to(k_pe, (B, S, MLA_HEADS, QK_ROPE_DIM))], axis=-1)
    o = causal_block_attention(q_full, k_full, v, (QK_NOPE_DIM + QK_ROPE_DIM) ** -0.5)
    return o.reshape(B, S, MLA_WIDTH)


def setup_inputs(seed: int = 0) -> dict:
    key = jax.random.key(seed)
    ks = iter(jax.random.split(key, 40))
    L, D = DEPTH, D_MODEL
    nrm = lambda shape, scale: jax.random.normal(next(ks), shape, jnp.float32) * scale
    gain = lambda shape: 1.0 + nrm(shape, 0.02)
    x = nrm((BATCH, SEQ, D), 1.0)
    c = nrm((BATCH, D), 1.0)
    positions = (jnp.arange(SEQ, dtype=jnp.int32)[None, :]
                 + jax.random.randint(next(ks), (BATCH, 1), 0, 1024, dtype=jnp.int32))
    return {
        'x': x, 'c': c, 'positions': positions,
        'w_in': nrm((L, D, IN_COLS), D ** -0.5),
        'mu_shift': jax.random.uniform(next(ks), (L, RWKV_COLS), jnp.float32),
        'w0': -1.0 + nrm((L, RWKV_WIDTH), 0.5),
        'w_decay_up': nrm((L, DECAY_LORA, RWKV_WIDTH), DECAY_LORA ** -0.5),
        'a0': nrm((L, RWKV_WIDTH), 0.1),
        'w_aaa_up': nrm((L, AAA_LORA, RWKV_WIDTH), AAA_LORA ** -0.5),
        'w_gate_up': nrm((L, GATE_LORA, RWKV_WIDTH), GATE_LORA ** -0.5),
        'k_k': 0.85 + nrm((L, RWKV_WIDTH), 0.05),
        'k_a': gain((L, RWKV_WIDTH)),
        'r_k': nrm((L, RWKV_HEADS, RWKV_HEAD_DIM), 0.1),
        'ln_x_g': gain((L, RWKV_WIDTH)),
        'ln_x_b': nrm((L, RWKV_WIDTH), 0.02),
        'b_forget': 2.0 + nrm((L, FOX_HEADS), 0.5),
        'q_norm_g': gain((L, Q_LORA_RANK)),
        'w_q_up': nrm((L, Q_LORA_RANK, MLA_HEADS * (QK_NOPE_DIM + QK_ROPE_DIM)), Q_LORA_RANK ** -0.5),
        'kv_norm_g': gain((L, KV_LORA_RANK)),
        'w_kv_up': nrm((L, KV_LORA_RANK, MLA_HEADS * (QK_NOPE_DIM + V_HEAD_DIM)), KV_LORA_RANK ** -0.5),
        'w_branch_a': nrm((L, RWKV_WIDTH, D), RWKV_WIDTH ** -0.5),
        'w_branch_b': nrm((L, FOX_WIDTH, D), FOX_WIDTH ** -0.5),
        'w_branch_c': nrm((L, MLA_WIDTH, D), MLA_WIDTH ** -0.5),
        'w_out': nrm((L, D, D), D ** -0.5),
        'w_mod': nrm((L, D, 6 * D), 0.5 * D ** -0.5),
        'b_mod': nrm((L, 6 * D), 0.02),
        'norm_mix_pre': gain((L, D)),
        'norm_mix_post': gain((L, D)),
        'norm_ffn_pre': gain((L, D)),
        'norm_ffn_post': gain((L, D)),
        'w_ffn_up': nrm((L, D, D_FF), D ** -0.5),
        'w_ffn_down': nrm((L, D_FF, D), D_FF ** -0.5),
    }


def reference(x, c, positions, w_in, mu_shift, w0, w_decay_up, a0, w_aaa_up, w_gate_up, k_k, k_a, r_k,
              ln_x_g, ln_x_b, b_forget, q_norm_g, w_q_up, kv_norm_g, w_kv_up, w_branch_a, w_branch_b,
              w_branch_c, w_out, w_mod, b_mod, norm_mix_pre, norm_mix_post, norm_ffn_pre, norm_ffn_post,
              w_ffn_up, w_ffn_down):
    B, S, D = x.shape
    inv_freq = ROPE_THETA ** (-jnp.arange(0, QK_ROPE_DIM, 2, dtype=jnp.float32) / QK_ROPE_DIM)
    ang = positions.astype(jnp.float32)[..., None] * inv_freq
    cos, sin = jnp.cos(ang), jnp.sin(ang)
    c_act = jax.nn.silu(c)
    split_cols = [RWKV_COLS, RWKV_COLS + FOX_COLS, RWKV_COLS + FOX_COLS + MLA_COLS]
    for l in range(DEPTH):
        mod = c_act @ w_mod[l] + b_mod[l]
        sh_m, sc_m, g_m, sh_f, sc_f, g_f = [m[:, None, :] for m in jnp.split(mod, 6, axis=-1)]

        h = rms_norm(x, norm_mix_pre[l]) * (1 + sc_m) + sh_m
        z = h @ w_in[l]
        z_a, z_b, z_c, z_g = jnp.split(z, split_cols, axis=-1)
        z_a = z_a + (token_shift(z_a) - z_a) * mu_shift[l]
        y_a = rwkv7_mix(z_a, w0[l], w_decay_up[l], a0[l], w_aaa_up[l], w_gate_up[l], k_k[l], k_a[l],
                        r_k[l], ln_x_g[l], ln_x_b[l])
        y_b = fox_mix(z_b, b_forget[l])
        y_c = mla_mix(z_c, q_norm_g[l], w_q_up[l], kv_norm_g[l], w_kv_up[l], cos, sin)
        gates = jax.nn.sigmoid(z_g).reshape(B, S, N_BRANCHES, D)
        merged = (gates[:, :, 0] * (y_a @ w_branch_a[l]) + gates[:, :, 1] * (y_b @ w_branch_b[l])
                  + gates[:, :, 2] * (y_c @ w_branch_c[l]))
        x = x + g_m * rms_norm(merged @ w_out[l], norm_mix_post[l])

        h = rms_norm(x, norm_ffn_pre[l]) * (1 + sc_f) + sh_f
        u = jnp.square(jax.nn.relu(h @ w_ffn_up[l]))
        x = x + g_f * rms_norm(u @ w_ffn_down[l], norm_ffn_post[l])
    return x
```

```python
import numpy as np
from contextlib import ExitStack
import concourse.bass as bass
import concourse.mybir as mybir
from concourse.bass_utils import run_bass_kernel_spmd
from concourse.alu_op_type import AluOpType as ALU

F32 = mybir.dt.float32
BF16 = mybir.dt.bfloat16
I32 = mybir.dt.int32
AF = mybir.ActivationFunctionType

S = 4096
D = 1024
KC = 8
TT = 512
NT = S // TT
L_FULL = 4
NV = 107
ARENA_BYTES = 196608
LW = 0.6065306597126334


class Prog:
    QUEUES = ('pe', 'act', 'dve', 'pool', 'sp')

    def __init__(self):
        self.ops = {q: [] for q in self.QUEUES}
        self.count = {}
        self.last_write = {}
        self.readers = {}
        self.known = {q: {} for q in self.QUEUES}
        self.needed = {}

    def add(self, queue, fn, reads=(), writes=(), dma_key=None):
        is_dma = dma_key is not None
        stream = ('dma:' + str(dma_key)) if is_dma else queue
        deps = {}

        def dep(tok, raw):
            if tok is None:
                return
            s, i = tok
            if s == stream and not is_dma:
                if queue == 'pe' or not raw:
                    return
            if deps.get(s, -1) < i:
                deps[s] = i
        for r in reads:
            dep(self.last_write.get(r), True)
        for w in writes:
            dep(self.last_write.get(w), False)
            for s, i in self.readers.get(w, {}).items():
                dep((s, i), False)
        if is_dma and self.count.get(stream, 0) > 0:
            deps[stream] = self.count[stream] - 1
        waits = []
        kn = self.known[queue]
        for s, i in deps.items():
            if kn.get(s, -1) >= i:
                continue
            kn[s] = i
            waits.append((s, i))
            self.needed.setdefault(s, set()).add(i)
        idx = self.count.get(stream, 0)
        self.count[stream] = idx + 1
        for w in writes:
            self.last_write[w] = (stream, idx)
            self.readers[w] = {}
        for r in reads:
            self.readers.setdefault(r, {})[stream] = idx
        self.ops[queue].append((fn, waits, stream, idx, is_dma))

    def barrier(self):
        for q in self.QUEUES:
            waits = []
            for s, n in self.count.items():
                if n == 0 or (s == q and q == 'pe'):
                    continue
                i = n - 1
                if self.known[q].get(s, -1) >= i:
                    continue
                self.known[q][s] = i
                waits.append((s, i))
                self.needed.setdefault(s, set()).add(i)
            if waits:
                self.ops[q].append((None, waits, None, None, False))
        self.last_write = {}
        self.readers = {}

    def emit(self, nc):
        streams = sorted(self.count.keys())
        vals = {}
        for s in streams:
            if s.startswith('dma:'):
                continue
            vals[s] = {i: r + 1 for r, i in enumerate(sorted(self.needed.get(s, ())))}
        with ExitStack() as es:
            sems = {s: es.enter_context(nc.semaphore('s%d' % k)) for k, s in enumerate(streams)}
            block = es.enter_context(nc.Block())

            def wval(s, i):
                if s.startswith('dma:'):
                    return 16 * (i + 1)
                return vals[s][i]

            def replay(q):
                def body(e):
                    for fn, waits, stream, idx, is_dma in self.ops[q]:
                        for s, i in waits:
                            e.wait_ge(sems[s], wval(s, i))
                        if fn is None:
                            continue
                        ins = fn(e)
                        if is_dma:
                            ins.then_inc(sems[stream], 16)
                        elif idx in vals[stream]:
                            ins.then_inc(sems[stream], 1)
                return body
            block.tensor(replay('pe'))
            block.scalar(replay('act'))
            block.vector(replay('dve'))
            block.gpsimd(replay('pool'))
            block.sync(replay('sp'))


class Arena:
    def __init__(self, ap, nbytes):
        self.ap = ap
        self.n = nbytes
        self.top = 0
        self.peak = 0

    def alloc(self, free_shape, dtype):
        esz = 2 if dtype == BF16 else 4
        ne = int(np.prod(free_shape))
        nb = (ne * esz + 63) // 64 * 64
        off = self.top
        self.top += nb
        self.peak = max(self.peak, self.top)
        assert self.top <= self.n, ('SBUF arena overflow', self.top)
        v = self.ap[:, off // 4:(off + nb) // 4]
        if dtype != F32:
            v = v.bitcast(dtype)
        v = v[:, 0:ne]
        if len(free_shape) == 2:
            v = v.rearrange('p (a b) -> p a b', a=free_shape[0], b=free_shape[1])
        elif len(free_shape) == 3:
            v = v.rearrange('p (a b c) -> p a b c', a=free_shape[0], b=free_shape[1], c=free_shape[2])
        return v

    def mark(self):
        return self.top

    def release(self, m):
        self.top = m


def build(depth=L_FULL, debug=False, stop_after=None):
    nc = bass.Bass("TRN2", target_bir_lowering=False)
    P = Prog()

    def din(name, shape, dt=F32):
        return nc.dram_tensor(name, list(shape), dt, kind="ExternalInput").ap()

    x_d = din("x", [S, D])
    pos_d = din("pos", [1, S], I32)
    cvec_d = din("cvec", [8, 128])
    vecs_d = din("vecs", [L_FULL, NV, 128])
    bfg_d = din("bfg", [L_FULL, 4, 1])
    invf_d = din("invf", [64, 1])
    w_in_d = din("w_in", [L_FULL, D, 5572])
    wkpe_sw_d = din("wkpe_sw", [L_FULL, D, 64])
    wdu_d = din("wdu", [L_FULL, 64, 256])
    wau_d = din("wau", [L_FULL, 64, 256])
    wgu_d = din("wgu", [L_FULL, 128, 256])
    wq_d = din("wq", [L_FULL, 384, 768])
    wq_sw_d = din("wq_sw", [L_FULL, 384, 256])
    wkv_d = din("wkv", [L_FULL, 256, 1024])
    wba_d = din("wba", [L_FULL, 256, D])
    wbb_d = din("wbb", [L_FULL, 256, D])
    wbc_d = din("wbc", [L_FULL, 512, D])
    wout_d = din("wout", [L_FULL, D, D])
    wmod_d = din("wmod", [L_FULL, D, 6 * D])
    wup_d = din("wup", [L_FULL, D, 4 * D])
    wdn_d = din("wdn", [L_FULL, 4 * D, D])
    out_d = nc.dram_tensor("out", [S, D], F32, kind="ExternalOutput").ap()
    skind = "ExternalOutput" if debug else "Internal"
    xT_s = nc.dram_tensor("xT_s", [KC, 128, S], F32, kind=skind).ap()
    yT_s = nc.dram_tensor("yT_s", [KC, 128, S], BF16, kind=skind).ap()
    mT_s = nc.dram_tensor("mT_s", [KC, 128, S], BF16, kind=skind).ap()
    hT_s = nc.dram_tensor("hT_s", [KC, 128, S], BF16, kind=skind).ap()
    cs_s = nc.dram_tensor("cs_s", [2, 64, S], BF16, kind=skind).ap()
    fo_s = nc.dram_tensor("fo_s", [KC, 128, S], F32, kind=skind).ap()

    es = ExitStack()
    arena_t = es.enter_context(nc.sbuf_tensor("arena", [128, ARENA_BYTES // 4], F32))
    A = Arena(arena_t[:, :], ARENA_BYTES)
    PS = [es.enter_context(nc.psum_tensor("ps%d" % i, [128, 512], F32))[:, :] for i in range(8)]
    PK = ['ps%d' % i for i in range(8)]

    def MM(out, lhsT, rhs, start=True, stop=True, r=(), w=()):
        P.add('pe', lambda e: e.matmul(out, lhsT=lhsT, rhs=rhs, start=start, stop=stop), r, w)

    def TR(out, in_, ident, r=(), w=()):
        P.add('pe', lambda e: e.transpose(out, in_, ident), r, w)

    def ACT(out, in_, func, r=(), w=(), scale=None, bias=None):
        kw = {}
        if scale is not None:
            kw['scale'] = scale
        if bias is not None:
            kw['bias'] = bias
        P.add('act', lambda e: e.activation(out=out, in_=in_, func=func, **kw), r, w)

    def TTo(eng, out, in0, in1, op, r=(), w=()):
        P.add(eng, lambda e: e.tensor_tensor(out=out, in0=in0, in1=in1, op=op), r, w)

    def TS(eng, out, in0, s1, s2, op0, op1=None, r=(), w=()):
        if op1 is None:
            P.add(eng, lambda e: e.tensor_scalar(out=out, in0=in0, scalar1=s1, scalar2=None, op0=op0), r, w)
        else:
            P.add(eng, lambda e: e.tensor_scalar(out=out, in0=in0, scalar1=s1, scalar2=s2, op0=op0, op1=op1), r, w)

    def STT(out, in0, scalar, in1, op0, op1, r=(), w=()):
        P.add('dve', lambda e: e.scalar_tensor_tensor(out=out, in0=in0, scalar=scalar, in1=in1, op0=op0, op1=op1), r, w)

    def CP(eng, out, in_, r=(), w=()):
        if eng == 'act':
            P.add('act', lambda e: e.activation(out=out, in_=in_, func=AF.Copy), r, w)
        else:
            P.add(eng, lambda e: e.tensor_copy(out=out, in_=in_), r, w)

    def MEMSET(eng, ap, val, w=()):
        P.add(eng, lambda e: e.memset(ap, val), (), w)

    def DMA(out, in_, key, r=(), w=(), q='sp', slow=False):
        if slow:
            P.add(q, lambda e: e.dma_start(out=out, in_=in_, allow_slow_non_contiguous=True), r, w, dma_key=key)
        else:
            P.add(q, lambda e: e.dma_start(out=out, in_=in_), r, w, dma_key=key)

    def RMAX(out, in_, r=(), w=()):
        P.add('dve', lambda e: e.reduce_max(out=out, in_=in_, axis=mybir.AxisListType.X), r, w)

    def SCAN(out, d0, d1, init, r=(), w=()):
        P.add('dve', lambda e: e.tensor_tensor_scan(out=out, data0=d0, data1=d1, initial=init, op0=ALU.mult, op1=ALU.add), r, w)

    def ASEL(out, in_, pattern, cmp, fill, base, cm, r=(), w=()):
        P.add('pool', lambda e: e.affine_select(out=out, in_=in_, pattern=pattern, compare_op=cmp, fill=fill,
                                                base=base, channel_multiplier=cm), r, w)

    cnt = [0]

    def alt():
        cnt[0] += 1
        return 'act' if cnt[0] % 2 else 'dve'

    ident_f = A.alloc([128], F32)
    ident_b = A.alloc([128], BF16)
    ones_b = A.alloc([128], BF16)
    bd64 = A.alloc([128], F32)
    bd64m = A.alloc([128], F32)
    MU = A.alloc([512], F32)
    MUI = A.alloc([512], F32)
    ML = A.alloc([512], F32)
    IDR = A.alloc([512], F32)
    RESET = A.alloc([512], F32)
    MASKNEG = A.alloc([128], BF16)
    SEL = [A.alloc([128], BF16), A.alloc([128], BF16)]
    ones_f = A.alloc([512], F32)
    zeros_f = A.alloc([512], F32)
    vT = A.alloc([L_FULL, NV], F32)
    modT = A.alloc([L_FULL, 48], F32)

    MEMSET('pool', ones_f, 1.0, w=['ones_f'])
    MEMSET('pool', zeros_f, 0.0, w=['zeros_f'])
    MEMSET('dve', ones_b, 1.0, w=['ones_b'])
    ASEL(ident_f, ones_f[:, 0:128], [[-1, 128]], ALU.is_equal, 0.0, 0, 1, r=['ones_f'], w=['ident_f'])
    CP('dve', ident_b, ident_f, r=['ident_f'], w=['ident_b'])
    MEMSET('dve', bd64, 0.0, w=['bd64'])
    MEMSET('dve', bd64[0:64, 0:64], 1.0, w=['bd64'])
    MEMSET('dve', bd64[64:128, 64:128], 1.0, w=['bd64'])
    TS('dve', bd64m, bd64, 1.0 / 64.0, None, ALU.mult, r=['bd64'], w=['bd64m'])
    for hb in (0, 64):
        sl = slice(hb, hb + 64)
        pat = [[0, 8], [1, 64]]
        ASEL(MU[sl, :], ones_f[sl, :], pat, ALU.is_ge, 0.0, -1, -1, r=['ones_f'], w=['MU'])
        ASEL(MUI[sl, :], ones_f[sl, :], pat, ALU.is_ge, 0.0, 0, -1, r=['ones_f'], w=['MUI'])
        ASEL(ML[sl, :], ones_f[sl, :], [[0, 8], [-1, 64]], ALU.is_ge, 0.0, -1, 1, r=['ones_f'], w=['ML'])
        ASEL(IDR[sl, :], ones_f[sl, :], [[0, 8], [-1, 64]], ALU.is_equal, 0.0, 0, 1, r=['ones_f'], w=['IDR'])
    MEMSET('dve', RESET, 1.0, w=['RESET'])
    MEMSET('dve', RESET.rearrange('p (c j) -> p c j', j=64)[:, :, 0:1], 0.0, w=['RESET'])
    ASEL(MASKNEG, zeros_f[:, 0:128], [[1, 128]], ALU.is_ge, -30000.0, 0, -1, r=['zeros_f'], w=['MASKNEG'])
    for i in (0, 1):
        MEMSET('dve', SEL[i], 0.0, w=['SEL'])
        MEMSET('dve', SEL[i][64 * i:64 * i + 1, :], 1.0, w=['SEL'])
        MEMSET('dve', SEL[i][64 * i + 32:64 * i + 33, :], 1.0, w=['SEL'])

    STG = 2048
    stg = [A.alloc([STG], F32) for _ in range(2)]
    stg_i = [0]

    def load_cast(dst, src, a, n, key):
        per = max(1, STG // n)
        for a0 in range(0, a, per):
            a1 = min(a, a0 + per)
            sl = stg_i[0] % 2
            stg_i[0] += 1
            sv = stg[sl][:, 0:(a1 - a0) * n].rearrange('p (a b) -> p a b', b=n)
            DMA(sv, src[:, a0:a1, :], 'stg%d' % sl, w=['stg%d' % sl])
            CP('pool', dst[:, a0:a1, :], sv, r=['stg%d' % sl], w=[key])

    hs = [A.alloc([KC, TT], BF16) for _ in range(2)]
    hcnt = [0]

    def hissue(t):
        sl = hcnt[0] % 2
        hcnt[0] += 1
        DMA(hs[sl], hT_s[:, :, t * TT:(t + 1) * TT].rearrange('k p t -> p k t'), 'hs%d' % sl, r=['hT_s'], w=['hs%d' % sl])
        return sl

    class HLoop:
        def __init__(self):
            self.nxt = hissue(0)

        def get(self, t):
            cur = self.nxt
            if t + 1 < NT:
                self.nxt = hissue(t + 1)
            return hs[cur], 'hs%d' % cur

    def wview(wd, r0, r1, c0, c1):
        return wd[r0:r1, c0:c1].rearrange('(k p) n -> p k n', p=128)

    def rms_rstd(src, nk, n_feat, eps, sq, rstd, bank, rkeys, sqkey, rstdkey, ncols=TT):
        ACT(sq[:, 0:nk, 0:ncols], src, AF.Square, r=rkeys, w=[sqkey])
        for k in range(nk):
            MM(PS[bank][:, 0:ncols], ones_b, sq[:, k, 0:ncols], start=(k == 0), stop=(k == nk - 1),
               r=[sqkey, 'ones_b'], w=[PK[bank]])
        ACT(rstd[:, 0:ncols], PS[bank][:, 0:ncols], AF.Ln, r=[PK[bank]], w=[rstdkey], scale=1.0 / n_feat, bias=eps_t[:, 0:1] if eps == 1e-6 else None)
        ACT(rstd[:, 0:ncols], rstd[:, 0:ncols], AF.Exp, r=[rstdkey], w=[rstdkey], scale=-0.5)

    eps_t = A.alloc([2], F32)
    MEMSET('dve', eps_t[:, 0:1], 1e-6, w=['eps_t'])
    MEMSET('dve', eps_t[:, 1:2], 64e-5, w=['eps_t'])

    def phase_vec_mod():
        m = A.mark()
        vin = A.alloc([128], F32)
        for l in range(depth):
            DMA(vin[0:NV, :], vecs_d[l], 'vin', w=['vin'])
            TR(PS[0][:, 0:NV], vin[0:NV, :], ident_f[0:NV, 0:NV], r=['vin', 'ident_f'], w=[PK[0]])
            CP('dve', vT[:, l, :], PS[0][:, 0:NV], r=[PK[0]], w=['vT'])
        cin = A.alloc([128], F32)
        cact = A.alloc([8, 2], F32)
        DMA(cin[0:8, :], cvec_d, 'cin', w=['cin'])
        TR(PS[1][:, 0:8], cin[0:8, :], ident_f[0:8, 0:8], r=['cin', 'ident_f'], w=[PK[1]])
        ACT(cact[:, :, 0], PS[1][:, 0:8], AF.Silu, r=[PK[1]], w=['cact'])
        ACT(cact[:, :, 1], PS[1][:, 0:8], AF.Silu, r=[PK[1]], w=['cact'])
        wm = [A.alloc([8, 512], F32) for _ in range(2)]
        for l in range(depth):
            for g in range(12):
                sl = g % 2
                DMA(wm[sl], wview(wmod_d[l], 0, D, g * 512, (g + 1) * 512), 'wm%d' % sl, w=['wm%d' % sl])
                for oc in range(4):
                    col = (g * 4 + oc) * 2
                    for kc in range(KC):
                        MM(PS[2][:, col:col + 2], wm[sl][:, kc, oc * 128:(oc + 1) * 128], cact[:, kc, :],
                           start=(kc == 0), stop=(kc == KC - 1), r=['wm%d' % sl, 'cact'], w=[PK[2]])
            TTo('dve', modT[:, l, :], PS[2][:, 0:96].rearrange('p (a b) -> p a b', b=2)[:, :, 0], vT[:, l, 32:80], ALU.add,
                r=[PK[2], 'vT'], w=['modT'])
        P.barrier()
        A.release(m)

    dv = A.alloc([64], F32)

    def layer_vectors(l):
        v = vT[:, l, :]
        mo = modT[:, l, :]
        TS('dve', dv[:, 0:8], mo[:, 8:16], 1.0, None, ALU.add, r=['modT'], w=['dv'])
        TTo('dve', dv[:, 0:8], dv[:, 0:8], v[:, 0:8], ALU.mult, r=['dv', 'vT'], w=['dv'])
        CP('dve', dv[:, 8:16], mo[:, 0:8], r=['modT'], w=['dv'])
        TTo('dve', dv[:, 16:24], mo[:, 16:24], v[:, 8:16], ALU.mult, r=['modT', 'vT'], w=['dv'])
        TS('dve', dv[:, 24:32], mo[:, 32:40], 1.0, None, ALU.add, r=['modT'], w=['dv'])
        TTo('dve', dv[:, 24:32], dv[:, 24:32], v[:, 16:24], ALU.mult, r=['dv', 'vT'], w=['dv'])
        CP('dve', dv[:, 32:40], mo[:, 24:32], r=['modT'], w=['dv'])
        TTo('dve', dv[:, 40:48], mo[:, 40:48], v[:, 24:32], ALU.mult, r=['modT', 'vT'], w=['dv'])
        TS('dve', dv[:, 48:56], v[:, 80:88], -1.0, 1.0, ALU.mult, ALU.add, r=['vT'], w=['dv'])
        TS('dve', dv[:, 56:58], v[:, 94:96], -1.0, 1.0, ALU.mult, ALU.add, r=['vT'], w=['dv'])
    Gm, SHm, GPm, Gf, SHf, GPf, OMU, OMKA = (dv[:, 0:8], dv[:, 8:16], dv[:, 16:24], dv[:, 24:32], dv[:, 32:40],
                                            dv[:, 40:48], dv[:, 48:56], dv[:, 56:58])

    def phase_x0():
        m = A.mark()
        xin = [A.alloc([4, D], F32) for _ in range(2)]
        xt = [A.alloc([KC, TT], F32) for _ in range(2)]
        for t in range(NT):
            sl = t % 2
            DMA(xin[sl], x_d[t * TT:(t + 1) * TT, :].rearrange('(s p) d -> p s d', p=128), 'xin%d' % sl, w=['xin%d' % sl])
            for kc in range(KC):
                b = kc % 4
                for sub in range(4):
                    TR(PS[b][:, sub * 128:(sub + 1) * 128], xin[sl][:, sub, kc * 128:(kc + 1) * 128], ident_f,
                       r=['xin%d' % sl, 'ident_f'], w=[PK[b]])
                CP(alt(), xt[sl][:, kc, :], PS[b], r=[PK[b]], w=['xt%d' % sl])
            DMA(xT_s[:, :, t * TT:(t + 1) * TT].rearrange('k p t -> p k t'), xt[sl], 'xt%d' % sl, r=['xt%d' % sl], w=['xT_s'])
        P.barrier()
        A.release(m)

    def phase_rope():
        m = A.mark()
        cs_tab = A.alloc([2, S], BF16)
        posi = A.alloc([S], I32)
        posf = A.alloc([S], F32)
        ang = A.alloc([S], F32)
        kf = A.alloc([S], F32)
        ki = A.alloc([S], I32)
        invf = A.alloc([1], F32)
        DMA(posi[0:64, :], pos_d[0].partition_broadcast(64), 'posi', w=['posi'])
        DMA(invf[0:64, :], invf_d, 'invf', w=['invf'])
        CP('dve', posf[0:64, :], posi[0:64, :], r=['posi'], w=['posf'])
        TS('dve', ang[0:64, :], posf[0:64, :], invf[0:64, 0:1], None, ALU.mult, r=['posf', 'invf'], w=['ang'])
        two_pi = 2.0 * np.pi
        c1 = 6.28125
        c2 = two_pi - c1
        for which in (0, 1):
            shift = (np.pi / 2.0) if which == 0 else 0.0
            TS('dve', kf[0:64, :], ang[0:64, :], shift, 1.0 / two_pi, ALU.add, ALU.mult, r=['ang'], w=['kf'])
            CP('dve', ki[0:64, :], kf[0:64, :], r=['kf'], w=['ki'])
            CP('dve', kf[0:64, :], ki[0:64, :], r=['ki'], w=['kf'])
            STT(posf[0:64, :], kf[0:64, :], -c1, ang[0:64, :], ALU.mult, ALU.add, r=['kf', 'ang'], w=['posf'])
            STT(posf[0:64, :], kf[0:64, :], -c2, posf[0:64, :], ALU.mult, ALU.add, r=['kf', 'posf'], w=['posf'])
            TS('dve', posf[0:64, :], posf[0:64, :], shift, None, ALU.add, r=['posf'], w=['posf'])
            TS('dve', kf[0:64, :], posf[0:64, :], float(np.pi), -two_pi, ALU.is_gt, ALU.mult, r=['posf'], w=['kf'])
            TTo('dve', posf[0:64, :], posf[0:64, :], kf[0:64, :], ALU.add, r=['posf', 'kf'], w=['posf'])
            TS('dve', kf[0:64, :], posf[0:64, :], -float(np.pi), two_pi, ALU.is_lt, ALU.mult, r=['posf'], w=['kf'])
            TTo('dve', posf[0:64, :], posf[0:64, :], kf[0:64, :], ALU.add, r=['posf', 'kf'], w=['posf'])
            TS('dve', posf[0:64, :], posf[0:64, :], 3.14159, -3.14159, ALU.min, ALU.max, r=['posf'], w=['posf'])
            if which == 0:
                ACT(cs_tab[0:64, 0, :], posf[0:64, :], AF.Sin, r=['posf'], w=['cs_tab'])
            else:
                ACT(cs_tab[32:64, 1, :], posf[32:64, :], AF.Sin, r=['posf'], w=['cs_tab'])
                ACT(cs_tab[0:32, 1, :], posf[0:32, :], AF.Sin, r=['posf'], w=['cs_tab'], scale=-1.0)
        for i in range(2):
            DMA(cs_s[i], cs_tab[0:64, i, :], 'cs_tab', r=['cs_tab'], w=['cs_s'])
        P.barrier()
        A.release(m)

    def phase_norm_h(l):
        m = A.mark()
        xt = [A.alloc([KC, TT], F32) for _ in range(2)]
        sq = A.alloc([KC, TT], BF16)
        rstd = A.alloc([TT], F32)
        tmp = A.alloc([TT], F32)
        ho = [A.alloc([KC, TT], BF16) for _ in range(2)]
        DMA(xt[0], xT_s[:, :, 0:TT].rearrange('k p t -> p k t'), 'xt0', r=['xT_s'], w=['xt0'])
        for t in range(NT):
            sl = t % 2
            if t + 1 < NT:
                DMA(xt[1 - sl], xT_s[:, :, (t + 1) * TT:(t + 2) * TT].rearrange('k p t -> p k t'), 'xt%d' % (1 - sl), r=['xT_s'], w=['xt%d' % (1 - sl)])
            rms_rstd(xt[sl], KC, D, 1e-6, sq, rstd, 0, ['xt%d' % sl], 'sq', 'rstd')
            for kc in range(KC):
                STT(tmp, xt[sl][:, kc, :], Gm[:, kc:kc + 1], rstd, ALU.mult, ALU.mult, r=['xt%d' % sl, 'dv', 'rstd'], w=['tmp'])
                ACT(ho[sl][:, kc, :], tmp, AF.Identity, r=['tmp', 'dv'], w=['ho%d' % sl], bias=SHm[:, kc:kc + 1])
            DMA(hT_s[:, :, t * TT:(t + 1) * TT].rearrange('k p t -> p k t'), ho[sl], 'ho%d' % sl, r=['ho%d' % sl], w=['hT_s'])
        P.barrier()
        A.release(m)

    def phase_rwkv(l):
        m = A.mark()
        v = vT[:, l, :]
        wr = A.alloc([KC, 1024], BF16)
        load_cast(wr, wview(w_in_d[l], 0, D, 0, 1024), KC, 1024, 'wr')
        wdu = A.alloc([256], F32)
        wau = A.alloc([256], F32)
        wgu = A.alloc([256], F32)
        DMA(wdu[0:64, :], wdu_d[l], 'wdu', w=['wdu'])
        DMA(wau[64:128, :], wau_d[l], 'wau', w=['wau'])
        DMA(wgu, wgu_d[l], 'wgu', w=['wgu'])
        zs = [A.alloc([TT + 1], F32) for _ in range(2)]
        carry = A.alloc([8], F32)
        for rc in range(8):
            MEMSET('dve', carry[:, rc:rc + 1], 0.0, w=['carry%d' % rc])
        za = A.alloc([8, TT], F32)
        tmp = A.alloc([TT], F32)
        tw = A.alloc([TT], F32)
        sgd = A.alloc([TT], F32)
        NW = 14
        W = [A.alloc([TT], F32) for _ in range(NW)]
        tok = [A.alloc([8, 64], F32) for _ in range(4)]
        NM = 12
        M_ = [A.alloc([TT], F32) for _ in range(NM)]
        ST = [[A.alloc([64], F32) for _ in range(2)] for _ in range(2)]
        Usb = A.alloc([64], F32)
        yo = [A.alloc([TT], BF16) for _ in range(2)]
        for hp in range(2):
            MEMSET('dve', ST[hp][0], 0.0, w=['ST%d_0' % hp])
        sti = [0, 0]
        K = lambda i: 'wk%d' % i
        TK = lambda i: 'tok%d' % i
        MK = lambda i: 'mt%d' % i
        HL = HLoop()

        for t in range(NT):
            ht, hk = HL.get(t)
            for rc in range(8):
                b = rc % 2
                for kc in range(KC):
                    MM(PS[b], wr[:, kc, rc * 128:(rc + 1) * 128], ht[:, kc, :], start=(kc == 0), stop=(kc == KC - 1),
                       r=['wr', hk], w=[PK[b]])
                z = zs[rc % 2]
                zk = 'zs%d' % (rc % 2)
                ck = 'carry%d' % rc
                CP('act', z[:, 1:TT + 1], PS[b], r=[PK[b]], w=[zk])
                CP('dve', z[:, 0:1], carry[:, rc:rc + 1], r=[ck], w=[zk])
                TS('dve', tmp, z[:, 0:TT], v[:, 80 + rc:81 + rc], None, ALU.mult, r=[zk, 'vT'], w=['tmpR'])
                STT(za[:, rc, :], z[:, 1:TT + 1], OMU[:, rc:rc + 1], tmp, ALU.mult, ALU.add, r=[zk, 'dv', 'tmpR'], w=['za%d' % rc])
                CP('dve', carry[:, rc:rc + 1], z[:, TT:TT + 1], r=[zk], w=[ck])
            ACT(tw[0:64, :], za[0:64, 6, :], AF.Tanh, r=['za6'], w=['tw'])
            ACT(sgd, za[:, 7, :], AF.Sigmoid, r=['za7'], w=['sgd'])
            for hp in range(2):
                sg, cs, epos, eneg, eexc, a_, kk, kp, At, Bt, Kt, Rt, bon, g_ = W
                r_ = za[:, 0 + hp, :]
                k_ = za[:, 2 + hp, :]
                v_ = za[:, 4 + hp, :]
                rk, kkey, vk = 'za%d' % hp, 'za%d' % (2 + hp), 'za%d' % (4 + hp)
                cols = slice(hp * 128, (hp + 1) * 128)
                MM(PS[2], wdu[0:64, cols], tw[0:64, :], r=['wdu', 'tw'], w=[PK[2]])
                ACT(sg, PS[2], AF.Sigmoid, r=[PK[2], 'vT'], w=[K(0)], bias=v[:, 88 + hp:89 + hp])
                SCAN(cs, RESET, sg, 0.0, r=['RESET', K(0)], w=[K(1)])
                ACT(epos, cs, AF.Exp, r=[K(1)], w=[K(2)], scale=-LW)
                ACT(eneg, cs, AF.Exp, r=[K(1)], w=[K(3)], scale=LW)
                TTo('dve', eexc, cs, sg, ALU.subtract, r=[K(1), K(0)], w=[K(4)])
                ACT(eexc, eexc, AF.Exp, r=[K(4)], w=[K(4)], scale=-LW)
                MM(PS[3], wau[64:128, cols], za[64:128, 6, :], r=['wau', 'za6'], w=[PK[3]])
                ACT(a_, PS[3], AF.Sigmoid, r=[PK[3], 'vT'], w=[K(5)], bias=v[:, 90 + hp:91 + hp])
                TS('dve', kk, k_, v[:, 92 + hp:93 + hp], None, ALU.mult, r=[kkey, 'vT'], w=[K(6)])
                TTo('dve', tmp, kk, kk, ALU.mult, r=[K(6)], w=['tmpR'])
                MM(PS[2], bd64, tmp, r=['bd64', 'tmpR'], w=[PK[2]])
                ACT(tmp, PS[2], AF.Ln, r=[PK[2], 'eps_t'], w=['tmpR'], bias=eps_t[:, 0:1])
                ACT(tmp, tmp, AF.Exp, r=['tmpR'], w=['tmpR'], scale=-0.5)
                TTo('dve', kk, kk, tmp, ALU.mult, r=[K(6), 'tmpR'], w=[K(6)])
                TS('dve', kp, a_, v[:, 94 + hp:95 + hp], OMKA[:, hp:hp + 1], ALU.mult, ALU.add, r=[K(5), 'vT', 'dv'], w=[K(7)])
                TTo('dve', kp, kp, k_, ALU.mult, r=[K(7), kkey], w=[K(7)])
                STT(At, kk, -1.0, eexc, ALU.mult, ALU.mult, r=[K(6), K(4)], w=[K(8)])
                TTo('dve', Bt, kk, a_, ALU.mult, r=[K(6), K(5)], w=[K(9)])
                TTo('dve', Bt, Bt, eneg, ALU.mult, r=[K(9), K(3)], w=[K(9)])
                TTo('dve', Kt, kp, eneg, ALU.mult, r=[K(7), K(3)], w=[K(10)])
                TTo('dve', Rt, r_, epos, ALU.mult, r=[rk, K(2)], w=[K(11)])
                STT(tmp, r_, v[:, 96 + hp:97 + hp], kp, ALU.mult, ALU.mult, r=[rk, 'vT', K(7)], w=['tmpR'])
                MM(PS[3], bd64, tmp, r=['bd64', 'tmpR'], w=[PK[3]])
                TTo('dve', bon, PS[3], v_, ALU.mult, r=[PK[3], vk], w=[K(12)])
                MM(PS[2], wgu[:, cols], sgd, r=['wgu', 'sgd'], w=[PK[2]])
                CP('act', g_, PS[2], r=[PK[2]], w=[K(13)])
                srcs = [(At, K(8)), (Bt, K(9)), (Kt, K(10)), (v_, vk)]
                for oi, (src, skey) in enumerate(srcs):
                    dstt = tok[oi]
                    dk = TK(oi)
                    for half in range(2):
                        b = 4 + half
                        for c4 in range(4):
                            c = half * 4 + c4
                            TR(PS[b][0:64, c4 * 128:(c4 + 1) * 128], src[:, c * 64:(c + 1) * 64], ident_f,
                               r=[skey, 'ident_f'], w=[PK[b]])
                        pv = PS[b][0:64, :].rearrange('p (c h f) -> p c h f', c=4, h=2)
                        CP('act', dstt[0:64, half * 4:half * 4 + 4, :], pv[:, :, 0, :], r=[PK[b]], w=[dk])
                        CP('dve', dstt[64:128, half * 4:half * 4 + 4, :], pv[:, :, 1, :], r=[PK[b]], w=[dk])
                Atok, Btok, Ktok, Vtok = tok
                X = [M_[0], M_[1]]
                XT = [M_[2], M_[3]]
                Tt = [M_[4], M_[5]]
                Nak, MrbT, MrkT, WmT, W0, U0 = M_[6:12]

                def bmm(bank, lhs_f, rhs_f, rk_):
                    for c in range(8):
                        for h2 in range(2):
                            pb = 64 * h2
                            MM(PS[bank][pb:pb + 64, c * 64:c * 64 + 64], lhs_f(c, pb), rhs_f(c, pb), r=rk_, w=[PK[bank]])
                fm = lambda T_: (lambda c, pb: T_[pb:pb + 64, c * 64:(c + 1) * 64])
                tm = lambda T_: (lambda c, pb: T_[pb:pb + 64, c, :])
                bmm(6, fm(Bt), fm(At), [K(9), K(8)])
                TTo('dve', X[0], PS[6], MU, ALU.mult, r=[PK[6], 'MU'], w=[MK(0)])
                bmm(7, fm(At), fm(Bt), [K(8), K(9)])
                TTo('dve', XT[0], PS[7], ML, ALU.mult, r=[PK[7], 'ML'], w=[MK(2)])
                bmm(6, fm(Kt), fm(At), [K(10), K(8)])
                TTo('dve', Nak, PS[6], MU, ALU.mult, r=[PK[6], 'MU'], w=[MK(6)])
                bmm(7, fm(Bt), fm(Rt), [K(9), K(11)])
                TTo('dve', MrbT, PS[7], MUI, ALU.mult, r=[PK[7], 'MUI'], w=[MK(7)])
                bmm(6, fm(Kt), fm(Rt), [K(10), K(11)])
                TTo('dve', MrkT, PS[6], MUI, ALU.mult, r=[PK[6], 'MUI'], w=[MK(8)])
                bmm(7, fm(Nak), tm(Vtok), [MK(6), TK(3)])
                CP('act', W0, PS[7], r=[PK[7]], w=[MK(10)])
                TTo('dve', Tt[0], X[0], IDR, ALU.add, r=[MK(0), 'IDR'], w=[MK(4)])
                ci, ti = 0, 0
                for lev in range(1, 6):
                    ni = 1 - ci
                    if lev < 5:
                        bmm(6, fm(XT[ci]), fm(X[ci]), [MK(2 + ci), MK(ci)])
                    bmm(7, fm(X[ci]), fm(XT[ci]), [MK(ci), MK(2 + ci)])
                    if lev < 5:
                        CP('act', X[ni], PS[6], r=[PK[6]], w=[MK(ni)])
                    CP('dve', XT[ni], PS[7], r=[PK[7]], w=[MK(2 + ni)])
                    ci = ni
                    bmm(6, fm(XT[ci]), fm(Tt[ti]), [MK(2 + ci), MK(4 + ti)])
                    TTo('dve', Tt[1 - ti], PS[6], Tt[ti], ALU.add, r=[PK[6], MK(4 + ti)], w=[MK(5 - ti)])
                    ti = 1 - ti
                TtF = Tt[ti]
                TtK = MK(4 + ti)
                bmm(7, tm(Atok), fm(TtF), [TK(0), TtK])
                CP('act', WmT, PS[7], r=[PK[7]], w=[MK(9)])
                bmm(6, fm(TtF), fm(W0), [TtK, MK(10)])
                CP('dve', U0, PS[6], r=[PK[6]], w=[MK(11)])
                ybank = 4 + hp
                for c in range(8):
                    so = ST[hp][sti[hp] % 2]
                    sn = ST[hp][(sti[hp] + 1) % 2]
                    sok = 'ST%d_%d' % (hp, sti[hp] % 2)
                    snk = 'ST%d_%d' % (hp, (sti[hp] + 1) % 2)
                    sti[hp] += 1
                    cs_ = slice(c * 64, (c + 1) * 64)
                    for h2 in range(2):
                        ps_ = slice(64 * h2, 64 * h2 + 64)
                        MM(PS[2][ps_, 0:64], WmT[ps_, cs_], so[ps_, :], r=[MK(9), sok], w=[PK[2]])
                    TTo('dve', Usb, PS[2][:, 0:64], U0[:, cs_], ALU.add, r=[PK[2], MK(11)], w=['Usb'])
                    for h2 in range(2):
                        ps_ = slice(64 * h2, 64 * h2 + 64)
                        MM(PS[3][ps_, 0:64], Btok[ps_, c, :], Usb[ps_, :], start=True, stop=False, r=[TK(1), 'Usb'], w=[PK[3]])
                        MM(PS[3][ps_, 0:64], Ktok[ps_, c, :], Vtok[ps_, c, :], start=False, stop=False, r=[TK(2), TK(3)], w=[PK[3]])
                        MM(PS[3][ps_, 0:64], ident_f[ps_, ps_], so[ps_, :], start=False, stop=True, r=['ident_f', sok], w=[PK[3]])
                        MM(PS[ybank][ps_, cs_], so[ps_, :], Rt[ps_, cs_], start=True, stop=False, r=[sok, K(11)], w=[PK[ybank]])
                        MM(PS[ybank][ps_, cs_], Usb[ps_, :], MrbT[ps_, cs_], start=False, stop=False, r=['Usb', MK(7)], w=[PK[ybank]])
                        MM(PS[ybank][ps_, cs_], Vtok[ps_, c, :], MrkT[ps_, cs_], start=False, stop=True, r=[TK(3), MK(8)], w=[PK[ybank]])
                    TS('dve', sn, PS[3][:, 0:64], epos[:, c * 64 + 63:c * 64 + 64], None, ALU.mult, r=[PK[3], K(2)], w=[snk])
                ysb = M_[0]
                CP('act', ysb, PS[ybank], r=[PK[ybank]], w=[MK(0)])
                MM(PS[6], bd64m, ysb, r=['bd64m', MK(0)], w=[PK[6]])
                TTo('dve', ysb, ysb, PS[6], ALU.subtract, r=[MK(0), PK[6]], w=[MK(0)])
                TTo('dve', M_[1], ysb, ysb, ALU.mult, r=[MK(0)], w=[MK(1)])
                MM(PS[7], bd64m, M_[1], r=['bd64m', MK(1)], w=[PK[7]])
                ACT(M_[1], PS[7], AF.Ln, r=[PK[7], 'eps_t'], w=[MK(1)], bias=eps_t[:, 1:2])
                ACT(M_[1], M_[1], AF.Exp, r=[MK(1)], w=[MK(1)], scale=-0.5)
                TTo('dve', ysb, ysb, M_[1], ALU.mult, r=[MK(0), MK(1)], w=[MK(0)])
                TS('dve', ysb, ysb, v[:, 98 + hp:99 + hp], v[:, 100 + hp:101 + hp], ALU.mult, ALU.add, r=[MK(0), 'vT'], w=[MK(0)])
                TTo('dve', ysb, ysb, bon, ALU.add, r=[MK(0), K(12)], w=[MK(0)])
                yk = 'yo%d' % hp
                TTo('dve', yo[hp], ysb, g_, ALU.mult, r=[MK(0), K(13)], w=[yk])
                DMA(yT_s[hp, :, t * TT:(t + 1) * TT], yo[hp], yk, r=[yk], w=['yT_s'], q='act')
        P.barrier()
        A.release(m)

    def attention(qk_mms, nbias_f, v_f, vkeys, dv_, ycol_dst, pt, ptk, yb, ybk, lnb, lnbk, banks):
        sb, ob, lb = banks
        blocks = []
        for qc in range(NT):
            nj = 4 * qc + 4
            for j in range(nj):
                blocks.append((qc, j, nj))

        def emit_qk(i):
            qc, j, nj = blocks[i]
            q0 = max(0, j - 4 * qc) * 128
            qk_mms(j, qc * TT + q0, (qc + 1) * TT, sb[i % 2], j >= 4 * qc)

        emit_qk(0)
        for i, (qc, j, nj) in enumerate(blocks):
            o_bank = ob[qc % 2]
            l_bank = lb[qc % 2]
            q0 = max(0, j - 4 * qc) * 128
            nq = TT - q0
            bank = sb[i % 2]
            p_ = pt[i % 3]
            pk_ = ptk[i % 3]
            ACT(p_[:, 0:nq], PS[bank][:, 0:nq], AF.Exp, r=[PK[bank], 'nb'], w=[pk_], bias=nbias_f(j))
            if i + 1 < len(blocks):
                emit_qk(i + 1)
            MM(PS[o_bank][0:dv_, q0:TT], v_f(j), p_[:, 0:nq], start=(j == 0), stop=(j == nj - 1), r=vkeys + [pk_], w=[PK[o_bank]])
            MM(PS[l_bank][0:dv_, q0:TT], ones_b[:, 0:dv_], p_[:, 0:nq], start=(j == 0), stop=(j == nj - 1), r=['ones_b', pk_], w=[PK[l_bank]])
            if j == nj - 1:
                ACT(lnb[0:dv_, :], PS[l_bank][0:dv_, :], AF.Ln, r=[PK[l_bank]], w=[lnbk])
                ACT(lnb[0:dv_, :], lnb[0:dv_, :], AF.Exp, r=[lnbk], w=[lnbk], scale=-1.0)
                y_ = yb[qc % 2]
                yk_ = ybk[qc % 2]
                TTo('dve', y_[0:dv_, :], PS[o_bank][0:dv_, :], lnb[0:dv_, :], ALU.mult, r=[PK[o_bank], lnbk], w=[yk_])
                ycol_dst(qc, y_, yk_)

    def phase_fox(l):
        sc = 64.0 ** -0.25
        for hp in range(2):
            m = A.mark()
            wq = A.alloc([KC, 128], BF16)
            wk_ = A.alloc([KC, 128], BF16)
            wv = A.alloc([KC, 128], BF16)
            wf = A.alloc([KC, 128], BF16)
            wfs = A.alloc([KC, 4], F32)
            c0 = 1024 + hp * 128
            load_cast(wq, wview(w_in_d[l], 0, D, c0, c0 + 128), KC, 128, 'wq')
            load_cast(wk_, wview(w_in_d[l], 0, D, c0 + 256, c0 + 384), KC, 128, 'wk')
            load_cast(wv, wview(w_in_d[l], 0, D, c0 + 512, c0 + 640), KC, 128, 'wv')
            MEMSET('pool', wf, 0.0, w=['wf'])
            DMA(wfs, wview(w_in_d[l], 0, D, 1792, 1796), 'wfs', w=['wfs'])
            for h2 in range(2):
                CP('pool', wf[:, :, 64 * h2:64 * h2 + 1], wfs[:, :, 2 * hp + h2:2 * hp + h2 + 1], r=['wfs'], w=['wf'])
            bfc = A.alloc([1], F32)
            MEMSET('dve', bfc, 0.0, w=['bfc'])
            for h2 in range(2):
                DMA(bfc[64 * h2:64 * h2 + 1, :], bfg_d[l, 2 * hp + h2:2 * hp + h2 + 1, :], 'bfc', r=['bfc'], w=['bfc'])
            IND = A.alloc([128], BF16)
            MEMSET('dve', IND, 0.0, w=['IND'])
            MEMSET('dve', IND[0:64, 0:1], 1.0, w=['IND'])
            MEMSET('dve', IND[64:128, 64:65], 1.0, w=['IND'])
            qT = A.alloc([S], BF16)
            kT = A.alloc([S], BF16)
            QA = A.alloc([S], BF16)
            Vt = A.alloc([32, 128], BF16)
            ncum = A.alloc([32, 2], F32)
            kmax = A.alloc([2], F32)
            MEMSET('dve', kmax, 0.0, w=['kmax'])
            MEMSET('pool', QA, 0.0, w=['QA'])
            sq = A.alloc([TT], BF16)
            rw = [A.alloc([TT], F32) for _ in range(4)]
            cumc = A.alloc([1], F32)
            MEMSET('dve', cumc, 0.0, w=['cumc'])
            HL = HLoop()
            for t in range(NT):
                ts_ = slice(t * TT, (t + 1) * TT)
                ht, hk = HL.get(t)
                for kc in range(KC):
                    MM(PS[0], wq[:, kc, :], ht[:, kc, :], start=(kc == 0), stop=(kc == KC - 1), r=['wq', hk], w=[PK[0]])
                ACT(qT[:, ts_], PS[0], AF.Copy, r=[PK[0]], w=['qT'], scale=sc)
                for kc in range(KC):
                    MM(PS[1], wk_[:, kc, :], ht[:, kc, :], start=(kc == 0), stop=(kc == KC - 1), r=['wk', hk], w=[PK[1]])
                ACT(kT[:, ts_], PS[1], AF.Copy, r=[PK[1]], w=['kT'], scale=sc)
                ACT(sq, qT[:, ts_], AF.Square, r=['qT'], w=['sqF'])
                MM(PS[2], IND, sq, r=['IND', 'sqF'], w=[PK[2]])
                CP('dve', rw[0], PS[2], r=[PK[2]], w=['rw0'])
                ACT(sq, kT[:, ts_], AF.Square, r=['kT'], w=['sqF'])
                MM(PS[3], IND, sq, r=['IND', 'sqF'], w=[PK[3]])
                RMAX(rw[3][:, 0:1], PS[3], r=[PK[3]], w=['rw3'])
                TTo('dve', kmax[:, 0:1], kmax[:, 0:1], rw[3][:, 0:1], ALU.max, r=['kmax', 'rw3'], w=['kmax'])
                for kc in range(KC):
                    MM(PS[2], wf[:, kc, :], ht[:, kc, :], start=(kc == 0), stop=(kc == KC - 1), r=['wf', hk], w=[PK[2]])
                ACT(rw[1], PS[2], AF.Sigmoid, r=[PK[2], 'bfc'], w=['rw1'], bias=bfc[:, 0:1])
                ACT(rw[1], rw[1], AF.Ln, r=['rw1'], w=['rw1'])
                SCAN(rw[2], ones_f, rw[1], cumc[:, 0:1], r=['ones_f', 'rw1', 'cumc'], w=['rw2'])
                CP('dve', cumc[:, 0:1], rw[2][:, TT - 1:TT], r=['rw2'], w=['cumc'])
                STT(rw[0], rw[0], -0.5, rw[2], ALU.mult, ALU.add, r=['rw0', 'rw2'], w=['rw0'])
                for h2 in range(2):
                    pr = 64 * h2
                    CP('dve', QA[pr:pr + 1, ts_], rw[0][pr:pr + 1, :], r=['rw0'], w=['QA'])
                    TTo('dve', rw[1][pr:pr + 1, :], rw[0][pr:pr + 1, :], QA[pr:pr + 1, ts_], ALU.subtract, r=['rw0', 'QA'], w=['rw1'])
                    CP('dve', QA[pr + 32:pr + 33, ts_], rw[1][pr:pr + 1, :], r=['rw1'], w=['QA'])
                for sub in range(4):
                    TR(PS[3][:, sub * 128:(sub + 1) * 128], rw[2][:, sub * 128:(sub + 1) * 128], ident_f, r=['rw2', 'ident_f'], w=[PK[3]])
                pv = PS[3].rearrange('p (s c) -> p s c', s=4)
                for h2 in range(2):
                    TS('dve', ncum[:, t * 4:(t + 1) * 4, h2], pv[:, :, 64 * h2], -1.0, None, ALU.mult, r=[PK[3]], w=['ncum'])
                for sub in range(4):
                    b = 4 + sub % 2
                    for kc in range(KC):
                        MM(PS[b][:, 0:128], ht[:, kc, sub * 128:(sub + 1) * 128], wv[:, kc, :], start=(kc == 0), stop=(kc == KC - 1), r=[hk, 'wv'], w=[PK[b]])
                    CP(alt(), Vt[:, t * 4 + sub, :], PS[b][:, 0:128], r=[PK[b]], w=['Vt'])
            kmb = A.alloc([2], F32)
            kb16 = A.alloc([2], BF16)
            kmh = A.alloc([2], F32)
            CP('dve', kb16[:, 0:1], kmax[:, 0:1], r=['kmax'], w=['kb16'])
            CP('dve', kb16[:, 1:2], kmax[:, 0:1], r=['kmax'], w=['kb16'])
            for h2 in range(2):
                MM(PS[0][:, 2 * h2:2 * h2 + 2], SEL[h2], kb16, r=['SEL', 'kb16'], w=[PK[0]])
            CP('dve', kmb, PS[0][:, 0:4].rearrange('p (a b) -> p a b', b=2)[:, :, 0], r=[PK[0]], w=['kmb'])
            TS('dve', kmh, kmb, -0.5, None, ALU.mult, r=['kmb'], w=['kmh'])
            for h2 in range(2):
                TS('dve', ncum[:, :, h2], ncum[:, :, h2], kmh[:, h2:h2 + 1], None, ALU.add, r=['ncum', 'kmh'], w=['nb'])
            pt = [A.alloc([TT], BF16) for _ in range(3)]
            yb = [A.alloc([TT], BF16) for _ in range(2)]
            lnb = A.alloc([TT], F32)
            for h2 in range(2):
                ps_ = slice(64 * h2, 64 * h2 + 64)

                def qk(j, qlo, qhi, bank, diag, ps_=ps_, h2=h2):
                    n = qhi - qlo
                    MM(PS[bank][:, 0:n], kT[ps_, j * 128:(j + 1) * 128], qT[ps_, qlo:qhi], start=True, stop=False, r=['kT', 'qT'], w=[PK[bank]])
                    MM(PS[bank][:, 0:n], SEL[h2], QA[:, qlo:qhi], start=False, stop=not diag, r=['SEL', 'QA'], w=[PK[bank]])
                    if diag:
                        MM(PS[bank][:, 0:128], ident_b, MASKNEG, start=False, stop=True, r=['ident_b', 'MASKNEG'], w=[PK[bank]])

                def ydst(qc, y_, yk_, h2=h2):
                    DMA(yT_s[2 + hp, 64 * h2:64 * h2 + 64, qc * TT:(qc + 1) * TT], y_[0:64, :], yk_, r=[yk_], w=['yT_s'], q='act')
                attention(qk, lambda j, h2=h2: ncum[:, j, h2:h2 + 1], lambda j, h2=h2: Vt[:, j, 64 * h2:64 * h2 + 64], ['Vt'], 64, ydst,
                          pt, ['ptF0', 'ptF1', 'ptF2'], yb, ['ybF0', 'ybF1'], lnb, 'lnbF', ([0, 1], [2, 3], [4, 5]))
            P.barrier()
            A.release(m)

    def phase_mla(l):
        sc = 192.0 ** -0.25
        m = A.mark()
        v = vT[:, l, :]
        qn = A.alloc([3, S], BF16)
        kvn = A.alloc([2, S], BF16)
        KPE = A.alloc([S], BF16)
        ksq_pe = A.alloc([S], BF16)
        cst = [A.alloc([2, TT], BF16) for _ in range(2)]
        t1 = A.alloc([TT], F32)
        t2 = A.alloc([TT], F32)
        sqb = A.alloc([TT], BF16)
        MEMSET('pool', KPE, 0.0, w=['KPE'])
        MEMSET('pool', KPE[64:65, :], 1.0, w=['KPE'])
        m1 = A.mark()
        wl = A.alloc([KC, 704], BF16)
        load_cast(wl, wview(w_in_d[l], 0, D, 1796, 2500), KC, 704, 'wl')
        wks = A.alloc([KC, 64], BF16)
        load_cast(wks, wview(wkpe_sw_d[l], 0, D, 0, 64), KC, 64, 'wks')
        lat = A.alloc([3, TT], F32)
        sq = A.alloc([3, TT], BF16)
        rstd = A.alloc([TT], F32)
        HL = HLoop()
        for t in range(NT):
            ts_ = slice(t * TT, (t + 1) * TT)
            ht, hk = HL.get(t)
            cs_ = cst[t % 2]
            csk = 'cst%d' % (t % 2)
            DMA(cs_[0:64, :, :], cs_s[:, :, ts_].rearrange('a p t -> p a t'), csk, r=['cs_s'], w=[csk])
            for c in range(3):
                b = c % 2
                for kc in range(KC):
                    MM(PS[b], wl[:, kc, c * 128:(c + 1) * 128], ht[:, kc, :], start=(kc == 0), stop=(kc == KC - 1), r=['wl', hk], w=[PK[b]])
                CP(alt(), lat[:, c, :], PS[b], r=[PK[b]], w=['lat'])
            rms_rstd(lat, 3, 384, 1e-6, sq, rstd, 2, ['lat'], 'sqC', 'rstdC')
            for c in range(3):
                STT(qn[:, c, ts_], lat[:, c, :], v[:, 102 + c:103 + c], rstd, ALU.mult, ALU.mult, r=['lat', 'vT', 'rstdC'], w=['qn'])
            for c in range(2):
                b = c % 2
                for kc in range(KC):
                    MM(PS[b], wl[:, kc, 384 + c * 128:384 + (c + 1) * 128], ht[:, kc, :], start=(kc == 0), stop=(kc == KC - 1), r=['wl', hk], w=[PK[b]])
                CP(alt(), lat[:, c, :], PS[b], r=[PK[b]], w=['lat'])
            rms_rstd(lat[:, 0:2, :], 2, 256, 1e-6, sq, rstd, 2, ['lat'], 'sqC', 'rstdC')
            for c in range(2):
                STT(kvn[:, c, ts_], lat[:, c, :], v[:, 105 + c:106 + c], rstd, ALU.mult, ALU.mult, r=['lat', 'vT', 'rstdC'], w=['kvn'])
            for kc in range(KC):
                MM(PS[3][0:64, :], wl[:, kc, 640:704], ht[:, kc, :], start=(kc == 0), stop=(kc == KC - 1), r=['wl', hk], w=[PK[3]])
            for kc in range(KC):
                MM(PS[4][0:64, :], wks[:, kc, :], ht[:, kc, :], start=(kc == 0), stop=(kc == KC - 1), r=['wks', hk], w=[PK[4]])
            TTo('dve', t1[0:64, :], PS[3][0:64, :], cs_[0:64, 0, :], ALU.mult, r=[PK[3], csk], w=['t1'])
            TTo('dve', t2[0:64, :], PS[4][0:64, :], cs_[0:64, 1, :], ALU.mult, r=[PK[4], csk], w=['t2'])
            TTo('dve', t1[0:64, :], t1[0:64, :], t2[0:64, :], ALU.add, r=['t1', 't2'], w=['t1'])
            ACT(KPE[0:64, ts_], t1[0:64, :], AF.Copy, r=['t1'], w=['KPE'], scale=sc)
            ACT(sqb[0:64, :], KPE[0:64, ts_], AF.Square, r=['KPE'], w=['sqb'])
            MM(PS[5][0:1, :], ones_b[0:64, 0:1], sqb[0:64, :], r=['ones_b', 'sqb'], w=[PK[5]])
            CP('dve', ksq_pe[0:1, ts_], PS[5][0:1, :], r=[PK[5]], w=['ksq_pe'])
        P.barrier()
        A.release(m1)
        wqu = A.alloc([3, 768], BF16)
        load_cast(wqu, wview(wq_d[l], 0, 384, 0, 768), 3, 768, 'wqu')
        wqs = A.alloc([3, 256], BF16)
        load_cast(wqs, wview(wq_sw_d[l], 0, 384, 0, 256), 3, 256, 'wqs')
        wkvu = A.alloc([2, 1024], BF16)
        load_cast(wkvu, wview(wkv_d[l], 0, 256, 0, 1024), 2, 1024, 'wkvu')
        qnT = A.alloc([S], BF16)
        QPE = A.alloc([S], BF16)
        knT = A.alloc([S], BF16)
        Vt = A.alloc([32, 128], BF16)
        nb = A.alloc([2], F32)
        kmax = A.alloc([2], F32)
        kb16 = A.alloc([2], BF16)
        row = A.alloc([TT], F32)
        sq1 = A.alloc([TT], BF16)
        pt = [A.alloc([TT], BF16) for _ in range(3)]
        yb = [A.alloc([TT], BF16) for _ in range(2)]
        lnb = A.alloc([TT], F32)
        MEMSET('pool', QPE, 0.0, w=['QPE'])
        cidx = 0
        for h in range(4):
            MEMSET('dve', kmax, 0.0, w=['kmaxC'])
            for t in range(NT):
                ts_ = slice(t * TT, (t + 1) * TT)
                cs_ = cst[cidx % 2]
                csk = 'cst%d' % (cidx % 2)
                cidx += 1
                DMA(cs_[0:64, :, :], cs_s[:, :, ts_].rearrange('a p t -> p a t'), csk, r=['cs_s'], w=[csk])
                for c in range(3):
                    MM(PS[0], wqu[:, c, h * 192:h * 192 + 128], qn[:, c, ts_], start=(c == 0), stop=(c == 2), r=['wqu', 'qn'], w=[PK[0]])
                ACT(qnT[:, ts_], PS[0], AF.Copy, r=[PK[0]], w=['qnT'], scale=sc)
                for c in range(3):
                    MM(PS[1][0:64, :], wqu[:, c, h * 192 + 128:h * 192 + 192], qn[:, c, ts_], start=(c == 0), stop=(c == 2), r=['wqu', 'qn'], w=[PK[1]])
                for c in range(3):
                    MM(PS[2][0:64, :], wqs[:, c, h * 64:(h + 1) * 64], qn[:, c, ts_], start=(c == 0), stop=(c == 2), r=['wqs', 'qn'], w=[PK[2]])
                TTo('dve', t1[0:64, :], PS[1][0:64, :], cs_[0:64, 0, :], ALU.mult, r=[PK[1], csk], w=['t1'])
                TTo('dve', t2[0:64, :], PS[2][0:64, :], cs_[0:64, 1, :], ALU.mult, r=[PK[2], csk], w=['t2'])
                TTo('dve', t1[0:64, :], t1[0:64, :], t2[0:64, :], ALU.add, r=['t1', 't2'], w=['t1'])
                ACT(QPE[0:64, ts_], t1[0:64, :], AF.Copy, r=['t1'], w=['QPE'], scale=sc)
                ACT(sqb, qnT[:, ts_], AF.Square, r=['qnT'], w=['sqb'])
                MM(PS[3][0:1, :], ones_b[:, 0:1], sqb, start=True, stop=False, r=['ones_b', 'sqb'], w=[PK[3]])
                ACT(sq1[0:64, :], QPE[0:64, ts_], AF.Square, r=['QPE'], w=['sq1'])
                MM(PS[3][0:1, :], ones_b[0:64, 0:1], sq1[0:64, :], start=False, stop=True, r=['ones_b', 'sq1'], w=[PK[3]])
                TS('dve', row[0:1, :], PS[3][0:1, :], -0.5, None, ALU.mult, r=[PK[3]], w=['row'])
                CP('dve', QPE[64:65, ts_], row[0:1, :], r=['row'], w=['QPE'])
                for c in range(2):
                    MM(PS[4], wkvu[:, c, h * 256:h * 256 + 128], kvn[:, c, ts_], start=(c == 0), stop=(c == 1), r=['wkvu', 'kvn'], w=[PK[4]])
                ACT(knT[:, ts_], PS[4], AF.Copy, r=[PK[4]], w=['knT'], scale=sc)
                ACT(sqb, knT[:, ts_], AF.Square, r=['knT'], w=['sqb'])
                MM(PS[5][0:1, :], ones_b[:, 0:1], sqb, r=['ones_b', 'sqb'], w=[PK[5]])
                TTo('dve', row[0:1, :], PS[5][0:1, :], ksq_pe[0:1, ts_], ALU.add, r=[PK[5], 'ksq_pe'], w=['row'])
                RMAX(kmax[0:1, 1:2], row[0:1, :], r=['row'], w=['kmaxC'])
                TTo('dve', kmax[0:1, 0:1], kmax[0:1, 0:1], kmax[0:1, 1:2], ALU.max, r=['kmaxC'], w=['kmaxC'])
                for sub in range(4):
                    tk_ = slice(t * TT + sub * 128, t * TT + (sub + 1) * 128)
                    b = 6 + sub % 2
                    for c in range(2):
                        MM(PS[b][:, 0:128], kvn[:, c, tk_], wkvu[:, c, h * 256 + 128:h * 256 + 256], start=(c == 0), stop=(c == 1), r=['kvn', 'wkvu'], w=[PK[b]])
                    CP(alt(), Vt[:, t * 4 + sub, :], PS[b][:, 0:128], r=[PK[b]], w=['VtC'])
            CP('dve', kb16[0:1, 0:1], kmax[0:1, 0:1], r=['kmaxC'], w=['kb16C'])
            CP('dve', kb16[0:1, 1:2], kmax[0:1, 0:1], r=['kmaxC'], w=['kb16C'])
            MM(PS[0][:, 0:2], ones_b[0:1, :], kb16[0:1, :], r=['ones_b', 'kb16C'], w=[PK[0]])
            TS('dve', nb, PS[0][:, 0:2], -0.5, None, ALU.mult, r=[PK[0]], w=['nb'])

            def qk(j, qlo, qhi, bank, diag):
                n = qhi - qlo
                MM(PS[bank][:, 0:n], knT[:, j * 128:(j + 1) * 128], qnT[:, qlo:qhi], start=True, stop=False, r=['knT', 'qnT'], w=[PK[bank]])
                MM(PS[bank][:, 0:n], KPE[:, j * 128:(j + 1) * 128], QPE[:, qlo:qhi], start=False, stop=not diag, r=['KPE', 'QPE'], w=[PK[bank]])
                if diag:
                    MM(PS[bank][:, 0:128], ident_b, MASKNEG, start=False, stop=True, r=['ident_b', 'MASKNEG'], w=[PK[bank]])

            def ydst(qc, y_, yk_, h=h):
                DMA(yT_s[4 + h, :, qc * TT:(qc + 1) * TT], y_, yk_, r=[yk_], w=['yT_s'], q='act')
            attention(qk, lambda j: nb[:, 0:1], lambda j: Vt[:, j, :], ['VtC'], 128, ydst,
                      pt, ['ptC0', 'ptC1', 'ptC2'], yb, ['ybC0', 'ybC1'], lnb, 'lnbC', ([0, 1], [2, 3], [4, 5]))
        P.barrier()
        A.release(m)

    def phase_merge(l):
        m = A.mark()
        yT = A.alloc([KC, S], BF16)
        for k in range(KC):
            DMA(yT[:, k, :], yT_s[k], 'yTl', r=['yT_s'], w=['yT'])
        wb = A.alloc([KC, D], BF16)
        load_cast(wb[:, 0:2, :], wview(wba_d[l], 0, 256, 0, D), 2, D, 'wb')
        load_cast(wb[:, 2:4, :], wview(wbb_d[l], 0, 256, 0, D), 2, D, 'wb')
        load_cast(wb[:, 4:8, :], wview(wbc_d[l], 0, 512, 0, D), 4, D, 'wb')
        wg = [A.alloc([KC, 3, 128], BF16) for _ in range(2)]
        sg = [A.alloc([TT], F32) for _ in range(2)]
        mg = [A.alloc([TT], F32) for _ in range(2)]
        mo = [A.alloc([TT], BF16) for _ in range(2)]
        brk = [(0, 2), (2, 4), (4, 8)]
        it = 0
        for dc in range(KC):
            sl = dc % 2
            for br in range(3):
                c0 = 2500 + br * 1024 + dc * 128
                load_cast(wg[sl][:, :, br, :], wview(w_in_d[l], 0, D, c0, c0 + 128), KC, 128, 'wg%d' % sl)
            HL = HLoop()
            for t in range(NT):
                ts_ = slice(t * TT, (t + 1) * TT)
                ht, hk = HL.get(t)
                ms = it % 2
                it += 1
                for br in range(3):
                    gb = (br % 2) * 2
                    pbk = gb + 1
                    for kc in range(KC):
                        MM(PS[gb], wg[sl][:, kc, br, :], ht[:, kc, :], start=(kc == 0), stop=(kc == KC - 1), r=['wg%d' % sl, hk], w=[PK[gb]])
                    k0, k1 = brk[br]
                    for k in range(k0, k1):
                        MM(PS[pbk], wb[:, k, dc * 128:(dc + 1) * 128], yT[:, k, ts_], start=(k == k0), stop=(k == k1 - 1), r=['wb', 'yT'], w=[PK[pbk]])
                    s_ = sg[br % 2]
                    sk = 'sgm%d' % (br % 2)
                    ACT(s_, PS[gb], AF.Sigmoid, r=[PK[gb]], w=[sk])
                    if br == 0:
                        TTo('dve', mg[ms], PS[pbk], s_, ALU.mult, r=[PK[pbk], sk], w=['mg%d' % ms])
                    else:
                        TTo('dve', s_, PS[pbk], s_, ALU.mult, r=[PK[pbk], sk], w=[sk])
                        if br == 1:
                            TTo('dve', mg[ms], mg[ms], s_, ALU.add, r=['mg%d' % ms, sk], w=['mg%d' % ms])
                        else:
                            TTo('dve', mo[ms], mg[ms], s_, ALU.add, r=['mg%d' % ms, sk], w=['mo%d' % ms])
                DMA(mT_s[dc, :, ts_], mo[ms], 'mo%d' % ms, r=['mo%d' % ms], w=['mT_s'], q='act')
        P.barrier()
        A.release(m)

    def phase_outproj(l):
        m = A.mark()
        wo = A.alloc([KC, D], BF16)
        load_cast(wo, wview(wout_d[l], 0, D, 0, D), KC, D, 'wo')
        mt_ = [A.alloc([KC, TT], BF16) for _ in range(2)]
        xt = [A.alloc([KC, TT], F32) for _ in range(2)]
        mo = A.alloc([KC, TT], F32)
        sq = A.alloc([KC, TT], BF16)
        rstd = A.alloc([TT], F32)
        tmp = A.alloc([TT], F32)

        def issue(t):
            sl = t % 2
            ts_ = slice(t * TT, (t + 1) * TT)
            DMA(mt_[sl], mT_s[:, :, ts_].rearrange('k p t -> p k t'), 'mtl%d' % sl, r=['mT_s'], w=['mtl%d' % sl])
            DMA(xt[sl], xT_s[:, :, ts_].rearrange('k p t -> p k t'), 'xt%d' % sl, r=['xT_s%d' % t], w=['xt%d' % sl])
        issue(0)
        for t in range(NT):
            sl = t % 2
            ts_ = slice(t * TT, (t + 1) * TT)
            if t + 1 < NT:
                issue(t + 1)
            for oc in range(KC):
                b = oc % 4
                for dc in range(KC):
                    MM(PS[b], wo[:, dc, oc * 128:(oc + 1) * 128], mt_[sl][:, dc, :], start=(dc == 0), stop=(dc == KC - 1), r=['wo', 'mtl%d' % sl], w=[PK[b]])
                CP(alt(), mo[:, oc, :], PS[b], r=[PK[b]], w=['moG'])
            rms_rstd(mo, KC, D, 1e-6, sq, rstd, 4, ['moG'], 'sqG', 'rstdG')
            for oc in range(KC):
                STT(tmp, mo[:, oc, :], GPm[:, oc:oc + 1], rstd, ALU.mult, ALU.mult, r=['moG', 'dv', 'rstdG'], w=['tmpG'])
                TTo('dve', xt[sl][:, oc, :], xt[sl][:, oc, :], tmp, ALU.add, r=['xt%d' % sl, 'tmpG'], w=['xt%d' % sl])
            DMA(xT_s[:, :, ts_].rearrange('k p t -> p k t'), xt[sl], 'xt%d' % sl, r=['xt%d' % sl], w=['xT_s%d' % t], q='act')
        P.barrier()
        A.release(m)

    def phase_ffn(l, last):
        for half in range(2):
            m = A.mark()
            wu = A.alloc([KC, 2 * D], BF16)
            wd = A.alloc([16, D], BF16)
            load_cast(wu, wview(wup_d[l], 0, D, half * 2 * D, (half + 1) * 2 * D), KC, 2 * D, 'wu')
            load_cast(wd, wview(wdn_d[l], half * 2 * D, (half + 1) * 2 * D, 0, D), 16, D, 'wd')
            rstd = A.alloc([TT], F32)
            tmp = A.alloc([TT], F32)
            ug = A.alloc([16, TT], BF16)
            ur = [A.alloc([TT], F32) for _ in range(2)]
            h2 = hs

            def up_down(t, hcur, hkey, evac):
                for fc in range(16):
                    b = 6 + fc % 2
                    for kc in range(KC):
                        MM(PS[b], wu[:, kc, fc * 128:(fc + 1) * 128], hcur[:, kc, :], start=(kc == 0), stop=(kc == KC - 1), r=['wu', hkey], w=[PK[b]])
                    ACT(ur[fc % 2], PS[b], AF.Relu, r=[PK[b]], w=['ur%d' % (fc % 2)])
                    TTo('dve', ug[:, fc, :], ur[fc % 2], ur[fc % 2], ALU.mult, r=['ur%d' % (fc % 2)], w=['ug%d' % fc])
                for oc in range(KC):
                    b = oc % 4
                    for fc in range(16):
                        MM(PS[b], wd[:, fc, oc * 128:(oc + 1) * 128], ug[:, fc, :], start=(fc == 0), stop=(fc == 15), r=['wd', 'ug%d' % fc], w=[PK[b]])
                    evac(oc, b)

            if half == 0:
                xt = [A.alloc([KC, TT], F32) for _ in range(2)]
                sq = A.alloc([KC, TT], BF16)
                fos = [A.alloc([TT], F32) for _ in range(2)]
                DMA(xt[0], xT_s[:, :, 0:TT].rearrange('k p t -> p k t'), 'xt0', r=['xT_s0'], w=['xt0'])
                for t in range(NT):
                    sl = t % 2
                    ts_ = slice(t * TT, (t + 1) * TT)
                    xk = 'xt%d' % sl
                    hk = 'hs%d' % sl
                    if t + 1 < NT:
                        DMA(xt[1 - sl], xT_s[:, :, (t + 1) * TT:(t + 2) * TT].rearrange('k p t -> p k t'), 'xt%d' % (1 - sl),
                            r=['xT_s%d' % (t + 1)], w=['xt%d' % (1 - sl)])
                    rms_rstd(xt[sl], KC, D, 1e-6, sq, rstd, 5, [xk], 'sqH', 'rstdH')
                    for kc in range(KC):
                        STT(tmp, xt[sl][:, kc, :], Gf[:, kc:kc + 1], rstd, ALU.mult, ALU.mult, r=[xk, 'dv', 'rstdH'], w=['tmpH'])
                        ACT(h2[sl][:, kc, :], tmp, AF.Identity, r=['tmpH', 'dv'], w=[hk], bias=SHf[:, kc:kc + 1])
                    DMA(hT_s[:, :, ts_].rearrange('k p t -> p k t'), h2[sl], hk, r=[hk], w=['hT_s%d' % t], q='act')

                    def evac0(oc, b, t=t, ts_=ts_):
                        f = fos[oc % 2]
                        fk = 'fos%d' % (oc % 2)
                        CP(alt(), f, PS[b], r=[PK[b]], w=[fk])
                        DMA(fo_s[oc, :, ts_], f, fk, r=[fk], w=['fo_s'], q='act')
                    up_down(t, h2[sl], hk, evac0)
            else:
                xt = A.alloc([KC, TT], F32)
                sq = A.alloc([KC, TT], BF16)
                prt = [A.alloc([TT], F32) for _ in range(2)]
                fo = A.alloc([KC, TT], F32)
                xo = A.alloc([2, D], F32)
                DMA(h2[0], hT_s[:, :, 0:TT].rearrange('k p t -> p k t'), 'hs0', r=['hT_s0'], w=['hs0'])
                pcnt = [0]
                for t in range(NT):
                    sl = t % 2
                    ts_ = slice(t * TT, (t + 1) * TT)
                    hk = 'hs%d' % sl
                    if t + 1 < NT:
                        DMA(h2[1 - sl], hT_s[:, :, (t + 1) * TT:(t + 2) * TT].rearrange('k p t -> p k t'), 'hs%d' % (1 - sl),
                            r=['hT_s%d' % (t + 1)], w=['hs%d' % (1 - sl)])
                    DMA(xt, xT_s[:, :, ts_].rearrange('k p t -> p k t'), 'xtH', r=['xT_s%d' % t], w=['xtH'])

                    def evac1(oc, b, ts_=ts_):
                        p = prt[pcnt[0] % 2]
                        pk = 'prt%d' % (pcnt[0] % 2)
                        pcnt[0] += 1
                        DMA(p, fo_s[oc, :, ts_], pk, r=['fo_s'], w=[pk])
                        TTo('dve', fo[:, oc, :], PS[b], p, ALU.add, r=[PK[b], pk], w=['fo'])
                    up_down(t, h2[sl], hk, evac1)
                    rms_rstd(fo, KC, D, 1e-6, sq, rstd, 5, ['fo'], 'sqH', 'rstdH')
                    for oc in range(KC):
                        STT(tmp, fo[:, oc, :], GPf[:, oc:oc + 1], rstd, ALU.mult, ALU.mult, r=['fo', 'dv', 'rstdH'], w=['tmpH'])
                        TTo('dve', xt[:, oc, :], xt[:, oc, :], tmp, ALU.add, r=['xtH', 'tmpH'], w=['xtH'])
                    if not last:
                        DMA(xT_s[:, :, ts_].rearrange('k p t -> p k t'), xt, 'xtH', r=['xtH'], w=['xT_s%d' % t], q='act')
                    else:
                        for sub in range(4):
                            for kc in range(KC):
                                b = 6 + kc // 4
                                TR(PS[b][:, (kc % 4) * 128:(kc % 4 + 1) * 128], xt[:, kc, sub * 128:(sub + 1) * 128], ident_f, r=['xtH', 'ident_f'], w=[PK[b]])
                            xs = sub % 2
                            CP('act', xo[:, xs, 0:512], PS[6], r=[PK[6]], w=['xo%d' % xs])
                            CP('dve', xo[:, xs, 512:1024], PS[7], r=[PK[7]], w=['xo%d' % xs])
                            DMA(out_d[t * TT + sub * 128:t * TT + (sub + 1) * 128, :], xo[:, xs, :], 'xo%d' % xs, r=['xo%d' % xs], w=['out'], q='act')
            P.barrier()
            A.release(m)

    phase_vec_mod()
    phase_x0()
    phase_rope()
    for l in range(depth):
        layer_vectors(l)
        phase_norm_h(l)
        if stop_after == 'norm':
            break
        phase_rwkv(l)
        if stop_after == 'rwkv':
            break
        phase_fox(l)
        if stop_after == 'fox':
            break
        phase_mla(l)
        if stop_after == 'mla':
            break
        phase_merge(l)
        phase_outproj(l)
        if stop_after == 'outproj':
            break
        phase_ffn(l, l == depth - 1)
    P.barrier()
    P.emit(nc)
    es.close()
    return nc, A.peak, {q: len(v) for q, v in P.ops.items()}


def prep_inputs(inp, b):
    f = lambda a: np.ascontiguousarray(np.asarray(a, dtype=np.float32))
    Lh = inp['w_in'].shape[0]
    vec_list = []
    for l in range(Lh):
        parts = [inp['norm_mix_pre'][l], inp['norm_mix_post'][l], inp['norm_ffn_pre'][l], inp['norm_ffn_post'][l],
                 inp['b_mod'][l], inp['mu_shift'][l], inp['w0'][l], inp['a0'][l], inp['k_k'][l], inp['k_a'][l],
                 np.asarray(inp['r_k'][l]).reshape(-1), inp['ln_x_g'][l], inp['ln_x_b'][l], inp['q_norm_g'][l], inp['kv_norm_g'][l]]
        vec_list.append(np.concatenate([np.asarray(p, dtype=np.float32).reshape(-1) for p in parts]).reshape(NV, 128))
    vecs = np.stack(vec_list)
    w_in = f(inp['w_in'])
    wkpe_sw = np.concatenate([w_in[:, :, 2436 + 32:2500], w_in[:, :, 2436:2436 + 32]], axis=-1)
    wq = f(inp['w_q_up'])
    wq4 = wq.reshape(Lh, 384, 4, 192)
    wq_sw = np.concatenate([wq4[..., 160:192], wq4[..., 128:160]], axis=-1).reshape(Lh, 384, 256)
    invf = (10000.0 ** (-np.arange(0, 64, 2, dtype=np.float32) / np.float32(64))).astype(np.float32)
    invf2 = np.concatenate([invf, invf]).reshape(64, 1)
    return {
        "x": f(inp['x'][b]), "pos": np.ascontiguousarray(np.asarray(inp['positions'][b], dtype=np.int32).reshape(1, S)),
        "cvec": f(inp['c'][b]).reshape(8, 128), "vecs": f(vecs), "bfg": f(inp['b_forget']).reshape(Lh, 4, 1),
        "invf": f(invf2), "w_in": w_in, "wkpe_sw": f(wkpe_sw), "wdu": f(inp['w_decay_up']), "wau": f(inp['w_aaa_up']),
        "wgu": f(inp['w_gate_up']), "wq": wq, "wq_sw": f(wq_sw), "wkv": f(inp['w_kv_up']),
        "wba": f(inp['w_branch_a']), "wbb": f(inp['w_branch_b']), "wbc": f(inp['w_branch_c']), "wout": f(inp['w_out']),
        "wmod": f(inp['w_mod']), "wup": f(inp['w_ffn_up']), "wdn": f(inp['w_ffn_down']),
    }


_NC_CACHE = {}


def kernel(**inputs):
    if 'nc' not in _NC_CACHE:
        _NC_CACHE['nc'] = build()[0]
    nc = _NC_CACHE['nc']
    maps = [prep_inputs(inputs, b) for b in range(4)]
    in_maps = [maps[i % 4] for i in range(8)]
    res = run_bass_kernel_spmd(nc, in_maps, core_ids=list(range(8)))
    out = np.stack([np.asarray(res.results[b]["out"], dtype=np.float32) for b in range(4)], axis=0)
    return out
```
